# Optimizing a Trainium2 kernel written in Bass

```python
import math
import jax
import jax.numpy as jnp
from jax import lax
import numpy as np

D_MODEL = 4096
BATCH = 1
SEQ = 16384
DEPTH = 4

W_GROUP = D_MODEL // 4
D_MIX = 4 * W_GROUP

HY_W = W_GROUP
HY_ORDER = 2
HY_BANDS = 16
HY_EMB = 2 * HY_BANDS + 1
HY_FILTER_WIDTH = 64
HY_INNER = 2
HY_SIN_W = 1.0
HY_TARGET = 1e-2
HY_SHORT_PCT = 0.3
HY_LONG_PCT = 1.5

RET_H = 4
RET_DV = W_GROUP // RET_H
RET_DK = RET_DV // 2
RET_CHUNK = 128

ATT_DH = 128
ATT_H = W_GROUP // ATT_DH
DIL_PAIRS = ((128, 1), (512, 4), (2048, 16))

ML_H = 4
ML_DV = W_GROUP // ML_H
ML_DK = ML_DV // 2
ML_CHUNK = 64
ML_GATES = 2 * 2 * ML_H

FFN_HIDDEN = ((-((-8 * D_MODEL) // 3) + 255) // 256) * 256

IN_SPLITS = (3 * HY_W,
             RET_H * RET_DK, RET_H * RET_DK, RET_H * RET_DV, RET_H * RET_DV,
             ATT_H * ATT_DH, ATT_H * ATT_DH, ATT_H * ATT_DH,
             ML_H * ML_DK, ML_H * ML_DK, ML_H * ML_DV, ML_H * ML_DV, ML_GATES)
D_IN = sum(IN_SPLITS)
NEG = -1e30
EPS = 1e-6

kernel_name = 'hybrid_bidir_parallel_heads_encoder'


def _split_points():
    pts, acc = [], 0
    for s in IN_SPLITS[:-1]:
        acc += s
        pts.append(acc)
    return pts


def rms_norm(x):
    xf = x.astype(jnp.float32)
    return (xf * lax.rsqrt(jnp.mean(xf * xf, -1, keepdims=True) + EPS)).astype(x.dtype)


def head_rms(x, gain=None):
    xf = x.astype(jnp.float32)
    y = xf * lax.rsqrt(jnp.mean(xf * xf, -1, keepdims=True) + EPS)
    return y if gain is None else y * gain.astype(jnp.float32)


def centred_conv3(u, w):
    up = jnp.pad(u, ((0, 0), (1, 1), (0, 0)))
    return up[:, :-2] * w[0] + up[:, 1:-1] * w[1] + up[:, 2:] * w[2]


def hyena_filters(L, w1, b1, w2, b2, w3):
    f32 = jnp.float32
    pos = jnp.arange(L, dtype=f32)
    t = jnp.linspace(0.0, 1.0, L, dtype=f32)
    ang = (2.0 * math.pi / L) * pos
    freqs = jnp.linspace(1e-4, HY_BANDS - 1, HY_BANDS, dtype=f32)
    z = jnp.concatenate([t[:, None], jnp.cos(ang[:, None] * freqs), -jnp.sin(ang[:, None] * freqs)], -1)
    h = jnp.sin(HY_SIN_W * (z @ w1.astype(f32) + b1.astype(f32)))
    for i in range(HY_INNER):
        h = jnp.sin(HY_SIN_W * (h @ w2[i].astype(f32) + b2[i].astype(f32)))
    h = (h @ w3.astype(f32)).reshape(L, HY_ORDER, 2, HY_W)
    max_decay = math.log(HY_TARGET) / HY_SHORT_PCT
    min_decay = math.log(HY_TARGET) / HY_LONG_PCT
    deltas = jnp.abs(jnp.linspace(min_decay, max_decay, HY_W, dtype=f32))
    h = h * jnp.exp(-t[:, None] * deltas)[:, None, None, :]
    fwd, bwd = h[:, :, 0], h[:, :, 1]
    full = jnp.concatenate([fwd, jnp.zeros((1, HY_ORDER, HY_W), f32), jnp.flip(bwd[1:], 0)], 0)
    full = full / jnp.sum(jnp.abs(full), 0, keepdims=True)
    return jnp.fft.rfft(full, axis=0)


def hyena_mixer(u, short_w, w1, b1, w2, b2, w3, bias):
    B, L, _ = u.shape
    u = centred_conv3(u, short_w)
    v, x1, x2 = jnp.split(u.astype(jnp.float32), 3, -1)
    kf = hyena_filters(L, w1, b1, w2, b2, w3)
    z = v
    for o, gate in enumerate((x1, x2)):
        zc = jnp.fft.irfft(jnp.fft.rfft(z, n=2 * L, axis=1) * kf[:, o], n=2 * L, axis=1)[:, :L]
        z = gate * (zc + bias[o].astype(jnp.float32) * z)
    return z


def retention_dir(q, k, v, log_gamma):
    B, H, L, dk = q.shape
    dv = v.shape[-1]
    C = RET_CHUNK
    N = L // C
    q = q.reshape(B, H, N, C, dk)
    k = k.reshape(B, H, N, C, dk)
    v = v.reshape(B, H, N, C, dv)
    pos = jnp.arange(C, dtype=jnp.float32)
    lg = log_gamma[:, None]
    diff = pos[:, None] - pos[None, :]
    intra_decay = jnp.exp(jnp.where(diff >= 0, lg[:, :, None] * diff, -jnp.inf))
    scores = jnp.einsum('bhnid,bhnjd->bhnij', q, k) * intra_decay[None, :, None]
    y = jnp.einsum('bhnij,bhnje->bhnie', scores, v)
    k_w = jnp.exp(lg * (C - 1 - pos))
    q_w = jnp.exp(lg * (pos + 1))
    kv = jnp.einsum('bhncd,bhnce->nbhde', k * k_w[None, :, None, :, None], v)
    chunk_decay = jnp.exp(log_gamma * C)[None, :, None, None]

    def step(state, kv_n):
        return state * chunk_decay + kv_n, state

    _, s_prev = lax.scan(step, jnp.zeros((B, H, dk, dv), jnp.float32), kv)
    y = y + jnp.einsum('bhncd,nbhde->bhnce', q * q_w[None, :, None, :, None], s_prev)
    return y.reshape(B, H, L, dv)


def retention_mixer(q, k, v, g, decay_logit):
    B, L, _ = q.shape
    heads = lambda a, d: a.reshape(B, L, RET_H, d).transpose(0, 2, 1, 3).astype(jnp.float32)
    q = heads(q, RET_DK)
    k = heads(k, RET_DK) * (RET_DK ** -0.5)
    v = heads(v, RET_DV)
    lg = jax.nn.log_sigmoid(decay_logit.astype(jnp.float32))
    flip = lambda a: jnp.flip(a, axis=2)
    y = retention_dir(q, k, v, lg[0]) + flip(retention_dir(flip(q), flip(k), flip(v), lg[1]))
    y = head_rms(y)
    y = y.transpose(0, 2, 1, 3).reshape(B, L, RET_H * RET_DV)
    return jax.nn.silu(g.astype(jnp.float32)) * y


def dilated_branch(q, k, v, window, dilation, slopes):
    B, L, H, Dh = q.shape
    steps = window // (2 * dilation)
    blk = steps
    n = L // dilation
    nb = -(-n // blk)
    pad = nb * blk - n
    strided = lambda a: a.reshape(B, n, dilation, H, Dh)
    qb = jnp.pad(strided(q), ((0, 0), (0, pad), (0, 0), (0, 0), (0, 0))).reshape(B, nb, blk, dilation, H, Dh)

    def band(a):
        ap = jnp.pad(strided(a), ((0, 0), (blk, pad + blk), (0, 0), (0, 0), (0, 0)))
        ap = ap.reshape(B, nb + 2, blk, dilation, H, Dh)
        return jnp.concatenate([ap[:, :-2], ap[:, 1:-1], ap[:, 2:]], axis=2)

    kb, vb = band(k), band(v)
    s = jnp.einsum('bnqrhd,bnkrhd->bnrhqk', qb, kb)
    qi = jnp.arange(blk)
    ki = jnp.arange(3 * blk)
    off = ki[None, :] - blk - qi[:, None]
    kpos = jnp.arange(nb)[:, None, None] * blk + ki[None, None, :] - blk
    valid = (jnp.abs(off) <= steps)[None] & (kpos >= 0) & (kpos < n)
    dist = (jnp.abs(off) * dilation).astype(jnp.float32)
    s = s - slopes[:, None, None] * dist
    s = jnp.where(valid[None, :, None, None], s, NEG)
    m = jnp.max(s, -1, keepdims=True)
    p = jnp.exp(s - m)
    den = jnp.sum(p, -1, keepdims=True)
    o = jnp.einsum('bnrhqk,bnkrhd->bnqrhd', p / den, vb)
    lse = jnp.transpose((m + jnp.log(den))[..., 0], (0, 1, 4, 2, 3))
    o = o.reshape(B, nb * blk, dilation, H, Dh)[:, :n].reshape(B, L, H, Dh)
    lse = lse.reshape(B, nb * blk, dilation, H)[:, :n].reshape(B, L, H)
    return o, lse


def dilated_attention(q, k, v, qk_gain):
    B, L, _ = q.shape
    slopes = 2.0 ** (-8.0 * jnp.arange(1, ATT_H + 1, dtype=jnp.float32) / ATT_H)
    q = head_rms(q.reshape(B, L, ATT_H, ATT_DH), qk_gain[0]) * (ATT_DH ** -0.5)
    k = head_rms(k.reshape(B, L, ATT_H, ATT_DH), qk_gain[1])
    v = v.reshape(B, L, ATT_H, ATT_DH).astype(jnp.float32)
    outs, lses = [], []
    for window, dilation in DIL_PAIRS:
        o, lse = dilated_branch(q, k, v, window, dilation, slopes)
        outs.append(o)
        lses.append(lse)
    w = jax.nn.softmax(jnp.stack(lses, 0), axis=0)
    o = jnp.sum(w[..., None] * jnp.stack(outs, 0), 0)
    return o.reshape(B, L, ATT_H * ATT_DH)


def mlstm_dir(q, k, v, ig, lf):
    B, H, L, dk = q.shape
    dv = v.shape[-1]
    C = ML_CHUNK
    N = L // C
    q = q.reshape(B, H, N, C, dk)
    k = k.reshape(B, H, N, C, dk)
    v = v.reshape(B, H, N, C, dv)
    ig = ig.reshape(B, H, N, C)
    b = jnp.cumsum(lf.reshape(B, H, N, C), -1)
    b_last = b[..., -1]
    a = b_last[..., None] - b + ig
    m_loc = jnp.max(a, -1)
    wa = jnp.exp(a - m_loc[..., None])
    kv = jnp.einsum('bhncd,bhnce->nbhde', k * wa[..., None], v)
    ksum = jnp.transpose(jnp.sum(k * wa[..., None], 3), (2, 0, 1, 3))

    def step(carry, xs):
        c_s, n_s, m_s = carry
        kv_n, k_n, mloc_n, blast_n = xs
        m_new = jnp.maximum(blast_n + m_s, mloc_n)
        sp = jnp.exp(blast_n + m_s - m_new)
        sc = jnp.exp(mloc_n - m_new)
        c_new = sp[..., None, None] * c_s + sc[..., None, None] * kv_n
        n_new = sp[..., None] * n_s + sc[..., None] * k_n
        return (c_new, n_new, m_new), (c_s, n_s, m_s)

    init = (jnp.zeros((B, H, dk, dv), jnp.float32), jnp.zeros((B, H, dk), jnp.float32),
            jnp.zeros((B, H), jnp.float32))
    _, (c_prev, n_prev, m_prev) = lax.scan(
        step, init, (kv, ksum, jnp.transpose(m_loc, (2, 0, 1)), jnp.transpose(b_last, (2, 0, 1))))
    inter_log = b + jnp.transpose(m_prev, (1, 2, 0))[..., None]
    idx = jnp.arange(C)
    lower = idx[:, None] >= idx[None, :]
    dlog = jnp.where(lower, b[..., :, None] - b[..., None, :] + ig[..., None, :], -jnp.inf)
    m_t = jnp.maximum(inter_log, jnp.max(dlog, -1))
    s = jnp.einsum('bhnid,bhnjd->bhnij', q, k) * jnp.exp(dlog - m_t[..., None])
    wi = jnp.exp(inter_log - m_t)
    num = jnp.einsum('bhnij,bhnje->bhnie', s, v) + wi[..., None] * jnp.einsum('bhncd,nbhde->bhnce', q, c_prev)
    den = jnp.sum(s, -1) + wi * jnp.einsum('bhncd,nbhd->bhnc', q, n_prev)
    h = num / jnp.maximum(jnp.abs(den), jnp.exp(-m_t))[..., None]
    return h.reshape(B, H, L, dv)


def mlstm_mixer(q, k, v, o, gates, gate_bias, norm_gain):
    B, L, _ = q.shape
    heads = lambda a, d: a.reshape(B, L, ML_H, d).transpose(0, 2, 1, 3).astype(jnp.float32)
    q = heads(q, ML_DK)
    k = heads(k, ML_DK) * (ML_DK ** -0.5)
    v = heads(v, ML_DV)
    g = (gates.astype(jnp.float32) + gate_bias.astype(jnp.float32)).reshape(B, L, 2, 2, ML_H)
    g = jnp.transpose(g, (2, 3, 0, 4, 1))
    ig = g[:, 0]
    lf = jax.nn.log_sigmoid(g[:, 1])
    flip = lambda a: jnp.flip(a, axis=2)
    h = mlstm_dir(q, k, v, ig[0], lf[0]) + flip(mlstm_dir(flip(q), flip(k), flip(v), flip(ig[1]), flip(lf[1])))
    h = head_rms(h, norm_gain.reshape(ML_H, 1, ML_DV))
    h = h.transpose(0, 2, 1, 3).reshape(B, L, ML_H * ML_DV)
    return jax.nn.sigmoid(o.astype(jnp.float32)) * h


def setup_inputs(seed: int = 0) -> dict:
    key = jax.random.key(seed)
    ks = jax.random.split(key, 21)
    f32 = jnp.float32
    nrm = lambda kk, shape, s: jax.random.normal(kk, shape, f32) * s
    x = nrm(ks[0], (BATCH, SEQ, D_MODEL), 1.0)
    c = nrm(ks[1], (BATCH, D_MODEL), 1.0)
    ada_w = nrm(ks[2], (D_MODEL, 6 * D_MODEL), 0.2 * D_MODEL ** -0.5)
    ada_b = nrm(ks[3], (6 * D_MODEL,), 0.02)
    ada_table = nrm(ks[4], (DEPTH, 6, D_MODEL), 0.1)
    w_in = nrm(ks[5], (DEPTH, D_MODEL, D_IN), D_MODEL ** -0.5)
    w_out = nrm(ks[6], (DEPTH, D_MIX, D_MODEL), D_MIX ** -0.5)
    hy_short = nrm(ks[7], (DEPTH, 3, 3 * HY_W), 3 ** -0.5)
    hy_w1 = nrm(ks[8], (DEPTH, HY_EMB, HY_FILTER_WIDTH), HY_EMB ** -0.5)
    hy_b1 = nrm(ks[9], (DEPTH, HY_FILTER_WIDTH), 0.1)
    hy_w2 = nrm(ks[10], (DEPTH, HY_INNER, HY_FILTER_WIDTH, HY_FILTER_WIDTH), HY_FILTER_WIDTH ** -0.5)
    hy_b2 = nrm(ks[11], (DEPTH, HY_INNER, HY_FILTER_WIDTH), 0.1)
    hy_w3 = nrm(ks[12], (DEPTH, HY_FILTER_WIDTH, HY_ORDER * 2 * HY_W), HY_FILTER_WIDTH ** -0.5)
    hy_bias = nrm(ks[13], (DEPTH, HY_ORDER, HY_W), 1.0)
    ret_base = jnp.log(2.0 ** (5.0 + jnp.arange(RET_H, dtype=f32)) - 1.0)
    ret_decay = ret_base + nrm(ks[14], (DEPTH, 2, RET_H), 0.05)
    att_qk_gain = 1.0 + nrm(ks[15], (DEPTH, 2, ATT_DH), 0.02)
    gate_base = jnp.stack([jnp.zeros((ML_H,), f32), jnp.linspace(3.0, 6.0, ML_H, dtype=f32)])
    ml_gate_bias = (gate_base[None, None] + nrm(ks[16], (DEPTH, 2, 2, ML_H), 0.1)).reshape(DEPTH, ML_GATES)
    ml_norm_gain = 1.0 + nrm(ks[17], (DEPTH, ML_H * ML_DV), 0.02)
    ffn_w1 = nrm(ks[18], (DEPTH, D_MODEL, FFN_HIDDEN), D_MODEL ** -0.5)
    ffn_w3 = nrm(ks[19], (DEPTH, D_MODEL, FFN_HIDDEN), D_MODEL ** -0.5)
    ffn_w2 = nrm(ks[20], (DEPTH, FFN_HIDDEN, D_MODEL), FFN_HIDDEN ** -0.5)
    return {'x': x, 'c': c, 'ada_w': ada_w, 'ada_b': ada_b, 'ada_table': ada_table,
            'w_in': w_in, 'w_out': w_out, 'hy_short': hy_short, 'hy_w1': hy_w1, 'hy_b1': hy_b1,
            'hy_w2': hy_w2, 'hy_b2': hy_b2, 'hy_w3': hy_w3, 'hy_bias': hy_bias,
            'ret_decay': ret_decay, 'att_qk_gain': att_qk_gain, 'ml_gate_bias': ml_gate_bias,
            'ml_norm_gain': ml_norm_gain, 'ffn_w1': ffn_w1, 'ffn_w3': ffn_w3, 'ffn_w2': ffn_w2}


def reference(x, c, ada_w, ada_b, ada_table, w_in, w_out, hy_short, hy_w1, hy_b1, hy_w2, hy_b2,
              hy_w3, hy_bias, ret_decay, att_qk_gain, ml_gate_bias, ml_norm_gain,
              ffn_w1, ffn_w3, ffn_w2):
    split_pts = _split_points()
    mod_shared = jax.nn.silu(c) @ ada_w + ada_b
    for l in range(DEPTH):
        mod = (mod_shared + ada_table[l].reshape(-1))[:, None, :]
        sh1, sc1, g1, sh2, sc2, g2 = jnp.split(mod, 6, -1)
        h = rms_norm(x) * (1 + sc1) + sh1
        proj = h @ w_in[l]
        (hy_in, r_q, r_k, r_v, r_g, a_q, a_k, a_v,
         m_q, m_k, m_v, m_o, m_g) = jnp.split(proj, split_pts, -1)
        y_a = hyena_mixer(hy_in, hy_short[l], hy_w1[l], hy_b1[l], hy_w2[l], hy_b2[l], hy_w3[l], hy_bias[l])
        y_b = retention_mixer(r_q, r_k, r_v, r_g, ret_decay[l])
        y_c = dilated_attention(a_q, a_k, a_v, att_qk_gain[l])
        y_d = mlstm_mixer(m_q, m_k, m_v, m_o, m_g, ml_gate_bias[l], ml_norm_gain[l])
        y = jnp.concatenate([y_a, y_b, y_c, y_d], -1).astype(x.dtype)
        x = x + g1 * (y @ w_out[l])
        h = rms_norm(x) * (1 + sc2) + sh2
        x = x + g2 * ((jax.nn.silu(h @ ffn_w1[l]) * (h @ ffn_w3[l])) @ ffn_w2[l])
    return x
```

```python
import math
import contextlib
import numpy as np
import ml_dtypes
import concourse.bass as bass
import concourse.mybir as mybir
from concourse.bass_utils import run_bass_kernel_spmd

F32 = mybir.dt.float32
BF16 = mybir.dt.bfloat16
AF = mybir.ActivationFunctionType
ALU = mybir.AluOpType
AX = mybir.AxisListType

D = 4096
L = 16384
DEPTH = 4
WG = 1024
D_IN = 12304
FH = 11008
EPS = 1e-6
NBF = ml_dtypes.bfloat16


class G:
    def __init__(self, nc, sem):
        self.nc = nc
        self.sem = sem
        self.cum = 0


class P:
    def __init__(self, g):
        self.g = g
        self.ops = []

    def add(self, eng, fn, inc=1, nowait=False):
        self.ops.append((eng, fn, inc, nowait))

    def mm(self, out, lhsT, rhs, start=True, stop=True, nowait=False):
        self.add('pe', lambda e: e.matmul(out, lhsT=lhsT, rhs=rhs, start=start, stop=stop), 1, nowait)

    def tr(self, out, in_, ident, nowait=False):
        self.add('pe', lambda e: e.transpose(out, in_, ident), 1, nowait)

    def dma(self, out, in_, nowait=False):
        self.add('sp', lambda e: e.dma_start(out=out, in_=in_), 16, nowait)

    def act(self, out, in_, func, bias=None, scale=None, accum_out=None):
        kw = {}
        if bias is not None:
            kw['bias'] = bias
        if scale is not None:
            kw['scale'] = scale
        if accum_out is not None:
            kw['accum_out'] = accum_out
        self.add('act', lambda e: e.activation(out, in_, func, **kw))

    def tt(self, out, a, b, op, eng='dve'):
        self.add(eng, lambda e: e.tensor_tensor(out, a, b, op))

    def ts(self, out, a, s1, s2, op0, op1=None, eng='dve', accum_out=None):
        kw = {}
        if accum_out is not None:
            kw['accum_out'] = accum_out
        if op1 is None:
            self.add(eng, lambda e: e.tensor_scalar(out, a, s1, None, op0, **kw))
        else:
            self.add(eng, lambda e: e.tensor_scalar(out, a, s1, s2, op0, op1, **kw))

    def stt(self, out, in0, scalar, in1, op0, op1, eng='dve'):
        self.add(eng, lambda e: e.scalar_tensor_tensor(out, in0, scalar, in1, op0, op1))

    def copy(self, out, in_, eng='dve'):
        if eng == 'act':
            self.add(eng, lambda e: e.copy(out, in_))
        else:
            self.add(eng, lambda e: e.tensor_copy(out, in_))

    def memset(self, ap, val, eng='dve'):
        self.add(eng, lambda e: e.memset(ap, val))

    def red(self, out, in_, op, eng='dve'):
        self.add(eng, lambda e: e.tensor_reduce(out, in_, AX.X, op))

    def scan(self, out, d0, d1, init, op0, op1):
        self.add('dve', lambda e: e.tensor_tensor_scan(out, d0, d1, init, op0, op1))

    def recip(self, out, in_):
        self.add('dve', lambda e: e.reciprocal(out, in_))

    def emit(self):
        g = self.g
        nc = g.nc
        plan = {'pe': [], 'act': [], 'dve': [], 'pool': [], 'sp': []}
        prev = None
        cum = g.cum
        for (eng, fn, inc, nowait) in self.ops:
            wait = None if (nowait and prev == eng) else cum
            plan[eng].append((wait, fn, inc))
            cum += inc
            prev = eng
        final = cum
        g.cum = cum
        sem = g.sem

        def run(e, lst):
            for wait, fn, inc in lst:
                if wait:
                    e.wait_ge(sem, wait)
                fn(e).then_inc(sem, inc)
            e.wait_ge(sem, final)

        with nc.Block() as block:
            @block.sync
            def _(e):
                run(e, plan['sp'])

            @block.tensor
            def _(e):
                run(e, plan['pe'])

            @block.scalar
            def _(e):
                run(e, plan['act'])

            @block.vector
            def _(e):
                run(e, plan['dve'])

            @block.gpsimd
            def _(e):
                run(e, plan['pool'])
        self.ops = []


class RowSplit:
    def __init__(self, aps, bounds):
        self.aps = aps
        self.bounds = bounds

    def pieces(self):
        for i, ap in enumerate(self.aps):
            yield self.bounds[i], self.bounds[i + 1] - self.bounds[i], ap

    def __getitem__(self, key):
        rk, ck = key
        if isinstance(rk, int):
            for base, n, ap in self.pieces():
                if base <= rk < base + n:
                    return ap[rk - base, ck]
        r0 = rk.start or 0
        r1 = rk.stop if rk.stop is not None else self.bounds[-1]
        for base, n, ap in self.pieces():
            if base <= r0 < base + n:
                assert r1 <= base + n, (r0, r1, base, n)
                return ap[r0 - base:r1 - base, ck]
        raise IndexError(key)


def as_split(x, rows):
    return x if isinstance(x, RowSplit) else RowSplit([x], [0, rows])


def dma_kc(p, tile, src, rows, c0, cw, store=False):
    first = True
    for base, n, ap in as_split(src, rows).pieces():
        k0 = base // 128
        view = ap[:, c0:c0 + cw].rearrange("(kc p) t -> p kc t", p=128)
        if store:
            p.dma(view, tile[:, k0:k0 + n // 128, :], nowait=not first)
        else:
            p.dma(tile[:, k0:k0 + n // 128, :], view, nowait=not first)
        first = False


_UID = [0]


def sb(stack, nc, name, shape, dt):
    _UID[0] += 1
    return stack.enter_context(nc.sbuf_tensor("%s_%d" % (name, _UID[0]), list(shape), dt))


def ps(stack, nc, name, shape, dt):
    _UID[0] += 1
    return stack.enter_context(nc.psum_tensor("%s_%d" % (name, _UID[0]), list(shape), dt))


def phase_cast(g, src, dst, K, N):
    nc = g.nc
    p = P(g)
    CW = 4096
    with contextlib.ExitStack() as st:
        a = sb(st, nc, "cast_a", [128, CW], F32)
        b = sb(st, nc, "cast_b", [128, CW], BF16)
        for r0 in range(0, K, 128):
            rs = min(128, K - r0)
            for c0 in range(0, N, CW):
                cs = min(CW, N - c0)
                p.dma(a[:rs, :cs], src[r0:r0 + rs, c0:c0 + cs])
                p.copy(b[:rs, :cs], a[:rs, :cs])
                p.dma(dst[r0:r0 + rs, c0:c0 + cs], b[:rs, :cs])
        p.emit()


def phase_mod(g, c_col, ada_w, ada_b_col, modsh):
    nc = g.nc
    p = P(g)
    with contextlib.ExitStack() as st:
        cc = sb(st, nc, "mod_c", [128, 32], F32)
        sg = sb(st, nc, "mod_sg", [128, 32], F32)
        w = sb(st, nc, "mod_w", [128, 32, 512], F32)
        res = sb(st, nc, "mod_res", [128, 192], F32)
        bb = sb(st, nc, "mod_b", [128, 192], F32)
        acc = ps(st, nc, "mod_ps", [128, 8], F32)
        p.dma(cc[:], c_col[:, :])
        p.dma(bb[:], ada_b_col[:, :])
        p.act(sg[:], cc[:], AF.Sigmoid)
        p.tt(cc[:], cc[:], sg[:], ALU.mult)
        for j0 in range(0, 192, 4):
            p.dma(w[:], ada_w[:, j0 * 128:(j0 + 4) * 128].rearrange("(kc p) n -> p kc n", p=128))
            for jj in range(4):
                for kc in range(32):
                    p.mm(acc[:, jj:jj + 1], w[:, kc, jj * 128:(jj + 1) * 128], cc[:, kc:kc + 1],
                         start=(kc == 0), stop=(kc == 31), nowait=(kc > 0))
            p.tt(res[:, j0:j0 + 4], acc[:, 0:4], bb[:, j0:j0 + 4], ALU.add)
        p.dma(modsh[:, :], res[:])
        p.emit()


def phase_norm(g, xT, hT, modl, sh_j, sc_j, Lc):
    nc = g.nc
    p = P(g)
    TB = 512
    with contextlib.ExitStack() as st:
        md = sb(st, nc, "nm_mod", [128, 192], F32)
        sc1 = sb(st, nc, "nm_sc1", [128, 32], F32)
        ones = sb(st, nc, "nm_ones", [128, 128], BF16)
        xt = sb(st, nc, "nm_x", [128, 32, TB], F32)
        sq = sb(st, nc, "nm_sq", [128, 32, TB], BF16)
        rstd = sb(st, nc, "nm_rstd", [128, TB], F32)
        tmp = sb(st, nc, "nm_tmp", [128, TB], F32)
        ho = sb(st, nc, "nm_h", [128, 32, TB], BF16)
        acc = ps(st, nc, "nm_ps", [128, TB], F32)
        p.dma(md[:], modl[:, :])
        p.memset(ones[:], 1.0)
        p.ts(sc1[:], md[:, sc_j:sc_j + 32], 1.0, None, ALU.add)
        for t0 in range(0, Lc, TB):
            dma_kc(p, xt, xT, D, t0, TB)
            p.act(sq[:], xt[:], AF.Square)
            for kc in range(32):
                p.mm(acc[:], ones[:], sq[:, kc, :], start=(kc == 0), stop=(kc == 31), nowait=(kc > 0))
            p.ts(rstd[:], acc[:], 1.0 / D, EPS, ALU.mult, ALU.add)
            p.act(rstd[:], rstd[:], AF.Ln)
            p.act(rstd[:], rstd[:], AF.Exp, scale=-0.5)
            for kc in range(32):
                eng = 'dve' if kc % 2 == 0 else 'pool'
                p.tt(tmp[:], xt[:, kc, :], rstd[:], ALU.mult)
                p.ts(ho[:, kc, :], tmp[:], sc1[:, kc:kc + 1], md[:, sh_j + kc:sh_j + kc + 1], ALU.mult, ALU.add)
            dma_kc(p, ho, hT, D, t0, TB, store=True)
        p.emit()


def phase_dense(g, actT, Wb, outT, K, N, Lc, kind, Wb2=None, resT=None, modl=None, gate_j=0, out_dt=F32):
    nc = g.nc
    p = P(g)
    KC = K // 128
    outT = as_split(outT, N)
    if resT is not None:
        resT = as_split(resT, N)
    TB = 1024 if Lc >= 1024 else Lc
    if KC > 32:
        TB = 512
    NT = 512
    MG = 256
    with contextlib.ExitStack() as st:
        A = sb(st, nc, "dn_A", [128, KC, TB], BF16)
        Wt = sb(st, nc, "dn_W", [128, KC, MG], BF16)
        Wt2 = sb(st, nc, "dn_W2", [128, KC, MG], BF16) if kind == 'swiglu' else None
        o = sb(st, nc, "dn_o", [128, TB], out_dt)
        tmp = sb(st, nc, "dn_tmp", [128, NT], F32)
        rs = sb(st, nc, "dn_res", [128, TB], F32) if kind == 'resid' else None
        md = sb(st, nc, "dn_mod", [128, 192], F32) if kind == 'resid' else None
        acc = ps(st, nc, "dn_ps", [128, NT], F32)
        acc2 = ps(st, nc, "dn_ps2", [128, NT], F32) if kind == 'swiglu' else None
        if kind == 'resid':
            p.dma(md[:], modl[:, :])
        for t0 in range(0, Lc, TB):
            dma_kc(p, A, actT, K, t0, TB)
            for m0 in range(0, N, MG):
                ms = min(MG, N - m0)
                p.dma(Wt[:, :, :ms], Wb[:, m0:m0 + ms].rearrange("(kc p) n -> p kc n", p=128))
                if kind == 'swiglu':
                    p.dma(Wt2[:, :, :ms], Wb2[:, m0:m0 + ms].rearrange("(kc p) n -> p kc n", p=128), nowait=True)
                for mm0 in range(0, ms, 128):
                    mw = min(128, ms - mm0)
                    mt = (m0 + mm0) // 128
                    if kind == 'resid':
                        p.dma(rs[:mw, :], resT[m0 + mm0:m0 + mm0 + mw, t0:t0 + TB])
                    for n0 in range(0, TB, NT):
                        for kc in range(KC):
                            p.mm(acc[:mw, :], Wt[:, kc, mm0:mm0 + mw], A[:, kc, n0:n0 + NT],
                                 start=(kc == 0), stop=(kc == KC - 1), nowait=(kc > 0))
                        if kind == 'swiglu':
                            for kc in range(KC):
                                p.mm(acc2[:mw, :], Wt2[:, kc, mm0:mm0 + mw], A[:, kc, n0:n0 + NT],
                                     start=(kc == 0), stop=(kc == KC - 1), nowait=True)
                            p.act(tmp[:mw, :], acc[:mw, :], AF.Silu)
                            p.tt(o[:mw, n0:n0 + NT], tmp[:mw, :], acc2[:mw, :], ALU.mult)
                        elif kind == 'resid':
                            p.stt(o[:mw, n0:n0 + NT], acc[:mw, :], md[:mw, gate_j + mt:gate_j + mt + 1],
                                  rs[:mw, n0:n0 + NT], ALU.mult, ALU.add)
                        else:
                            p.copy(o[:mw, n0:n0 + NT], acc[:mw, :], eng='act')
                    p.dma(outT[m0 + mm0:m0 + mm0 + mw, t0:t0 + TB], o[:mw, :])
        p.emit()


def make_consts():
    c = {}
    i = np.arange(128)
    c['ident'] = np.eye(128, dtype=np.float32)
    c['anti'] = np.eye(128, dtype=np.float32)[::-1].copy()
    c['ones'] = np.ones((128, 128), np.float32)
    c['maskF'] = (i[:, None] <= i[None, :]).astype(np.float32)
    c['maskB'] = (i[:, None] >= i[None, :]).astype(np.float32)
    lo = np.ones((128, 128), np.float32); lo[:64] = 0
    hi = np.ones((128, 128), np.float32); hi[64:] = 0
    c['ones_lo'] = lo
    c['ones_hi'] = hi
    return c

CONST_NAMES = ['ident', 'anti', 'ones', 'maskF', 'maskB', 'ones_lo', 'ones_hi']


def attn_bias_tables():
    j = np.arange(128)[:, None]
    cq = np.arange(256)[None, :]
    off = 64 + j - cq
    out = np.zeros((8, 3, 128, 256), np.float32)
    for h in range(8):
        slope = 2.0 ** (-8.0 * (h + 1) / 8)
        for b, r in enumerate((1, 4, 16)):
            out[h, b] = np.where(np.abs(off) <= 64, -slope * r * np.abs(off), -30000.0)
    return out


def phase_linattn(g, cst, projT, qrow, krow, vrow, kscale, gate_srcs, ydir, is_mlstm):
    nc = g.nc
    p = P(g)
    NCH = L // 128
    with contextlib.ExitStack() as st:
        qTb = sb(st, nc, "la_q", [128, L], BF16)
        kTb = sb(st, nc, "la_k", [128, L], BF16)
        vTb = sb(st, nc, "la_v", [128, 2, L], BF16)
        stg = sb(st, nc, "la_stg", [128, 2048], F32)
        ident = sb(st, nc, "la_id", [128, 128], F32)
        identb = sb(st, nc, "la_idb", [128, 128], BF16)
        ones = sb(st, nc, "la_ones", [128, 128], F32)
        Rm = sb(st, nc, "la_R", [128, 128], F32)
        mask = sb(st, nc, "la_mask", [128, 128], F32)
        Gi = sb(st, nc, "la_Gi", [128, 128], F32)
        Gf = sb(st, nc, "la_Gf", [128, 128], F32)
        bb = sb(st, nc, "la_bb", [128, 128], F32)
        uu = sb(st, nc, "la_u", [128, 128], F32)
        onesrow = sb(st, nc, "la_onesg", [128, 128], F32)
        tmpg = sb(st, nc, "la_tmpg", [128, 128], F32)
        cols = sb(st, nc, "la_cols", [128, 16], F32)
        rows = sb(st, nc, "la_rows", [1, 6, 128], F32)
        diag = sb(st, nc, "la_diag", [128, 128], F32)
        alc = sb(st, nc, "la_alc", [128, 128], F32)
        bec = sb(st, nc, "la_bec", [128, 128], F32)
        dec = sb(st, nc, "la_dec", [128, 128], F32)
        spb = sb(st, nc, "la_spb", [128, 128], F32)
        scb = sb(st, nc, "la_scb", [128, 128], F32)
        emb = sb(st, nc, "la_emb", [128, 128], F32)
        bias2 = sb(st, nc, "la_bias", [128, 2], F32)
        Pm = sb(st, nc, "la_Pm", [128, 128], BF16)
        ktk = sb(st, nc, "la_ktk", [128, 128], BF16)
        vp = sb(st, nc, "la_vp", [128, 257], BF16)
        t1 = sb(st, nc, "la_t1", [128, 257], F32)
        ot = sb(st, nc, "la_ot", [128, 257], F32)
        den = sb(st, nc, "la_den", [128, 2], F32)
        res = sb(st, nc, "la_res", [128, 256], F32)
        Cst = sb(st, nc, "la_C", [128, 257], F32)
        Cb = sb(st, nc, "la_Cb", [128, 257], BF16)
        ST = ps(st, nc, "la_ST", [128, 128], F32)
        O1 = ps(st, nc, "la_O1", [128, 257], F32)
        O2 = ps(st, nc, "la_O2", [128, 257], F32)
        KV = ps(st, nc, "la_KV", [128, 257], F32)
        trp = ps(st, nc, "la_trp", [128, 384], BF16)
        sm = ps(st, nc, "la_sm", [128, 128], F32)

        p.dma(ident[:], cst['ident'][:, :])
        p.dma(ones[:], cst['ones'][:, :])
        p.copy(identb[:], ident[:])
        p.memset(onesrow[:], 1.0)
        for c0 in range(0, L, 2048):
            p.dma(stg[:], projT[qrow:qrow + 128, c0:c0 + 2048])
            p.copy(qTb[:, c0:c0 + 2048], stg[:])
            p.dma(stg[:], projT[krow:krow + 128, c0:c0 + 2048])
            p.ts(kTb[:, c0:c0 + 2048], stg[:], float(kscale), None, ALU.mult)
            for hh in range(2):
                p.dma(stg[:], projT[vrow + hh * 128:vrow + hh * 128 + 128, c0:c0 + 2048])
                p.copy(vTb[:, hh, c0:c0 + 2048], stg[:])

        for d in range(2):
            ig_src, f_src, ig_bias, f_bias = gate_srcs[d]
            p.dma(Rm[:], (cst['ident'] if d == 0 else cst['anti'])[:, :])
            p.dma(mask[:], (cst['maskF'] if d == 0 else cst['maskB'])[:, :])
            if ig_src is not None:
                p.dma(Gi[:], ig_src.rearrange("(n i) -> n i", i=128))
                p.dma(bias2[:, 0:1], ig_bias.partition_broadcast(128))
                p.ts(Gi[:], Gi[:], bias2[:, 0:1], None, ALU.add)
            else:
                p.memset(Gi[:], 0.0)
            if f_src is not None:
                p.dma(Gf[:], f_src.rearrange("(n i) -> n i", i=128))
            else:
                p.memset(Gf[:], 0.0)
            p.dma(bias2[:, 1:2], f_bias.partition_broadcast(128))
            p.ts(bias2[:, 1:2], bias2[:, 1:2], -1.0, None, ALU.mult)
            p.act(Gf[:], Gf[:], AF.Exp, bias=bias2[:, 1:2], scale=-1.0)
            p.ts(Gf[:], Gf[:], 1.0, None, ALU.add)
            p.act(Gf[:], Gf[:], AF.Ln)
            p.ts(Gf[:], Gf[:], -1.0, None, ALU.mult)
            p.scan(bb[:], onesrow[:], Gf[:], 0.0, ALU.mult, ALU.add)
            p.copy(cols[:, 0:1], bb[:, 127:128])
            if d == 1:
                p.ts(bb[:], bb[:], -1.0, cols[:, 0:1], ALU.mult, ALU.add)
                p.tt(bb[:], bb[:], Gf[:], ALU.add)
            p.tt(uu[:], Gi[:], bb[:], ALU.subtract)
            p.red(cols[:, 1:2], uu[:], ALU.max)
            p.tt(cols[:, 2:3], cols[:, 0:1], cols[:, 1:2], ALU.add)
            p.mm(sm[0:1, 0:128], cols[:, 0:1], Rm[:])
            p.copy(rows[:, 0, :], sm[0:1, 0:128])
            p.mm(sm[0:1, 0:128], cols[:, 2:3], Rm[:])
            p.copy(rows[:, 1, :], sm[0:1, 0:128])
            p.scan(rows[:, 2, :], rows[:, 0, :], rows[:, 1, :], 0.0, ALU.add, ALU.max)
            p.memset(rows[:, 3, 0:1], 0.0)
            p.copy(rows[:, 3, 1:128], rows[:, 2, 0:127])
            p.mm(sm[:, 0:1], rows[:, 3, :], ones[0:1, 0:1])
            p.mm(sm[:, 1:2], rows[:, 2, :], ones[0:1, 0:1], nowait=True)
            p.copy(tmpg[:, 0:2], sm[:, 0:2])
            p.mm(sm[:, 0:2], Rm[:], tmpg[:, 0:2])
            p.copy(cols[:, 3:5], sm[:, 0:2])
            p.tt(cols[:, 5:6], cols[:, 3:4], cols[:, 1:2], ALU.max)
            p.ts(cols[:, 6:7], cols[:, 1:2], -1.0, None, ALU.mult)
            p.tt(cols[:, 7:8], cols[:, 1:2], cols[:, 5:6], ALU.subtract)
            p.tt(cols[:, 8:9], cols[:, 3:4], cols[:, 5:6], ALU.subtract)
            p.tt(cols[:, 9:10], cols[:, 0:1], cols[:, 3:4], ALU.add)
            p.tt(cols[:, 9:10], cols[:, 9:10], cols[:, 4:5], ALU.subtract)
            p.tt(cols[:, 10:11], cols[:, 2:3], cols[:, 4:5], ALU.subtract)
            p.ts(cols[:, 11:12], cols[:, 5:6], -1.0 if is_mlstm else 1.0, None, ALU.mult)
            p.act(cols[:, 9:12], cols[:, 9:12], AF.Exp)
            for (dst, src, bcol) in ((alc, uu, 6), (bec, bb, 7), (dec, bb, 8)):
                p.act(tmpg[:], src[:], AF.Exp, bias=cols[:, bcol:bcol + 1])
                p.tr(sm[:], tmpg[:], ident[:])
                p.copy(dst[:], sm[:])
            for (dst, ccol) in ((spb, 9), (scb, 10), (emb, 11)):
                p.ts(diag[:], ident[:], cols[:, ccol:ccol + 1], None, ALU.mult)
                p.mm(sm[:], ones[:], diag[:])
                p.copy(dst[:], sm[:])
            p.memset(Cst[:], 0.0)
            p.memset(Cb[:], 0.0)
            order = range(NCH) if d == 0 else range(NCH - 1, -1, -1)
            for n in order:
                cs = slice(n * 128, (n + 1) * 128)
                p.mm(ST[:], kTb[:, cs], qTb[:, cs])
                p.tr(trp[:, 0:128], kTb[:, cs], identb[:], nowait=True)
                p.tr(trp[:, 128:256], vTb[:, 0, cs], identb[:], nowait=True)
                p.tr(trp[:, 256:384], vTb[:, 1, cs], identb[:], nowait=True)
                p.tt(Pm[:], ST[:], mask[:], ALU.mult)
                p.copy(ktk[:], trp[:, 0:128], eng='act')
                p.ts(vp[:, 0:256], trp[:, 128:384], alc[:, n:n + 1], None, ALU.mult)
                p.copy(vp[:, 256:257], alc[:, n:n + 1])
                p.mm(O1[:], Pm[:], vp[:])
                p.mm(O2[:], qTb[:, cs], Cb[:], nowait=True)
                p.mm(KV[:], ktk[:], vp[:], nowait=True)
                p.act(t1[:], O1[:], AF.Copy, scale=bec[:, n:n + 1])
                p.stt(ot[:], O2[:], dec[:, n:n + 1], t1[:], ALU.mult, ALU.add)
                if is_mlstm:
                    p.ts(den[:, 1:2], ot[:, 256:257], -1.0, None, ALU.mult)
                    p.tt(den[:, 0:1], den[:, 1:2], ot[:, 256:257], ALU.max)
                    p.tt(den[:, 0:1], den[:, 0:1], emb[:, n:n + 1], ALU.max)
                    p.recip(den[:, 1:2], den[:, 0:1])
                    p.ts(res[:], ot[:, 0:256], den[:, 1:2], None, ALU.mult)
                else:
                    p.ts(res[:], ot[:, 0:256], emb[:, n:n + 1], None, ALU.mult)
                p.dma(ydir[d][n * 128:(n + 1) * 128, :], res[:])
                p.act(t1[:], KV[:], AF.Copy, scale=scb[:, n:n + 1])
                p.stt(Cst[:], Cst[:], spb[:, n:n + 1], t1[:], ALU.mult, ALU.add)
                p.copy(Cb[:], Cst[:], eng='act')
        p.emit()


def phase_combine(g, cst, projT, ydir, grow, yT, yrow, is_mlstm, gain_col):
    nc = g.nc
    p = P(g)
    with contextlib.ExitStack() as st:
        ident = sb(st, nc, "cb_id", [128, 128], F32)
        a = sb(st, nc, "cb_a", [128, 4, 256], F32)
        b = sb(st, nc, "cb_b", [128, 4, 256], F32)
        junk = sb(st, nc, "cb_junk", [128, 256], F32)
        ss = sb(st, nc, "cb_ss", [128, 4], F32)
        gt = sb(st, nc, "cb_g", [128, 2, 512], F32)
        gs = sb(st, nc, "cb_gs", [128, 2, 512], F32)
        o = sb(st, nc, "cb_o", [128, 2, 512], BF16)
        gn = sb(st, nc, "cb_gn", [128, 2], F32)
        tp = [ps(st, nc, "cb_tp%d" % i, [128, 512], F32) for i in range(2)]
        p.dma(ident[:], cst['ident'][:, :])
        if gain_col is not None:
            p.dma(gn[:], gain_col)
        for t0 in range(0, L, 512):
            p.dma(a[:], ydir[0][t0:t0 + 512, :].rearrange("(c p) e -> p c e", p=128))
            p.dma(b[:], ydir[1][t0:t0 + 512, :].rearrange("(c p) e -> p c e", p=128), nowait=True)
            for hh in range(2):
                p.dma(gt[:, hh, :], projT[grow + hh * 128:grow + hh * 128 + 128, t0:t0 + 512], nowait=True)
            p.tt(a[:], a[:], b[:], ALU.add)
            p.memset(ss[:], 0.0)
            for c in range(4):
                p.act(junk[:], a[:, c, :], AF.Square, accum_out=ss[:, c:c + 1])
            p.ts(ss[:], ss[:], 1.0 / 256, EPS, ALU.mult, ALU.add)
            p.act(ss[:], ss[:], AF.Ln)
            p.act(ss[:], ss[:], AF.Exp, scale=-0.5)
            for c in range(4):
                p.ts(a[:, c, :], a[:, c, :], ss[:, c:c + 1], None, ALU.mult)
            for c in range(4):
                for hh in range(2):
                    p.tr(tp[hh][:, c * 128:(c + 1) * 128], a[:, c, hh * 128:(hh + 1) * 128], ident[:],
                         nowait=(c + hh > 0))
            if is_mlstm:
                p.act(gs[:], gt[:], AF.Sigmoid)
            else:
                p.act(gs[:], gt[:], AF.Silu)
            for hh in range(2):
                if gain_col is not None:
                    p.ts(gs[:, hh, :], gs[:, hh, :], gn[:, hh:hh + 1], None, ALU.mult)
                p.tt(o[:, hh, :], gs[:, hh, :], tp[hh][:], ALU.mult)
                p.dma(yT[yrow + hh * 128:yrow + hh * 128 + 128, t0:t0 + 512], o[:, hh, :])
        p.emit()


def phase_attn(g, cst, projT, qrow, krow, vrow, gain_col, btab, yT, yrow):
    nc = g.nc
    p = P(g)
    PAD = 1024
    QB = 2048
    with contextlib.ExitStack() as st:
        qTb = sb(st, nc, "at_q", [128, L], BF16)
        kTb = sb(st, nc, "at_k", [128, L + 2 * PAD], BF16)
        vTb = sb(st, nc, "at_v", [128, L + 2 * PAD], BF16)
        stg = sb(st, nc, "at_stg", [128, 512], F32)
        sq = sb(st, nc, "at_sq", [128, 512], BF16)
        rstd = sb(st, nc, "at_rstd", [128, 512], F32)
        onesb = sb(st, nc, "at_ones", [128, 128], BF16)
        oneslo = sb(st, nc, "at_oneslo", [128, 128], BF16)
        oneshi = sb(st, nc, "at_oneshi", [128, 128], BF16)
        identb = sb(st, nc, "at_idb", [128, 128], BF16)
        cf = sb(st, nc, "at_cf", [128, 128], F32)
        gn = sb(st, nc, "at_gn", [128, 2], F32)
        tb = sb(st, nc, "at_tb", [128, 3, 256], F32)
        sc = sb(st, nc, "at_sc", [128, 128], F32)
        pT = sb(st, nc, "at_p", [128, 128], BF16)
        vt = sb(st, nc, "at_vt", [128, 128], BF16)
        num = sb(st, nc, "at_num", [128, QB], F32)
        den = sb(st, nc, "at_den", [128, QB], F32)
        ob = sb(st, nc, "at_ob", [128, QB], BF16)
        acc = ps(st, nc, "at_acc", [128, 512], F32)
        S = ps(st, nc, "at_S", [128, 128], F32)
        trp = ps(st, nc, "at_trp", [128, 128], BF16)
        Np = ps(st, nc, "at_N", [128, 128], F32)
        Dp = ps(st, nc, "at_D", [128, 128], F32)

        for (dst, nm) in ((onesb, 'ones'), (oneslo, 'ones_lo'), (oneshi, 'ones_hi'), (identb, 'ident')):
            p.dma(cf[:], cst[nm][:, :])
            p.copy(dst[:], cf[:])
        p.dma(gn[:], gain_col)
        p.ts(gn[:, 0:1], gn[:, 0:1], 128.0 ** -0.5, None, ALU.mult)
        p.dma(tb[:], btab.rearrange("b j c -> j b c"))
        p.memset(kTb[:, 0:PAD], 0.0)
        p.memset(kTb[:, PAD + L:], 0.0)
        p.memset(vTb[:, 0:PAD], 0.0)
        p.memset(vTb[:, PAD + L:], 0.0)
        for (row, dst, off, gcol) in ((qrow, qTb, 0, 0), (krow, kTb, PAD, 1)):
            for c0 in range(0, L, 512):
                p.dma(stg[:], projT[row:row + 128, c0:c0 + 512])
                p.act(sq[:], stg[:], AF.Square)
                p.mm(acc[:], onesb[:], sq[:])
                p.ts(rstd[:], acc[:], 1.0 / 128, EPS, ALU.mult, ALU.add)
                p.act(rstd[:], rstd[:], AF.Ln)
                p.act(rstd[:], rstd[:], AF.Exp, scale=-0.5)
                p.tt(stg[:], stg[:], rstd[:], ALU.mult)
                p.ts(dst[:, off + c0:off + c0 + 512], stg[:], gn[:, gcol:gcol + 1], None, ALU.mult)
        for c0 in range(0, L, 512):
            p.dma(stg[:], projT[vrow:vrow + 128, c0:c0 + 512])
            p.copy(vTb[:, PAD + c0:PAD + c0 + 512], stg[:])

        for t0 in range(0, L, QB):
            p.memset(num[:], 0.0)
            p.memset(den[:], 0.0)
            for b, r in enumerate((1, 4, 16)):
                n = L // r
                nq = QB // r // 128
                for rho in range(r):
                    for qi in range(nq):
                        it = t0 // r // 128 + qi
                        q0 = rho + r * 128 * it
                        qs = slice(q0, q0 + 127 * r + 1, r)
                        first = True
                        for m, chalf in ((it - 1, 1), (it, 0)):
                            k0 = PAD + rho + r * (128 * m + 64)
                            ks = slice(k0, k0 + 127 * r + 1, r)
                            p.mm(S[:], kTb[:, ks], qTb[:, qs])
                            p.tr(trp[:], vTb[:, ks], identb[:], nowait=True)
                            p.tt(sc[:], S[:], tb[:, b, chalf * 128:(chalf + 1) * 128], ALU.add)
                            p.act(pT[:], sc[:], AF.Exp)
                            p.copy(vt[:], trp[:])
                            if m == -1:
                                on = oneslo
                            elif m == n // 128 - 1:
                                on = oneshi
                            else:
                                on = onesb
                            p.mm(Np[:], vt[:], pT[:], start=first, stop=not first)
                            p.mm(Dp[:], on[:], pT[:], start=first, stop=not first, nowait=True)
                            first = False
                        ls = slice(q0 - t0, q0 - t0 + 127 * r + 1, r)
                        p.tt(num[:, ls], num[:, ls], Np[:], ALU.add)
                        p.tt(den[:, ls], den[:, ls], Dp[:], ALU.add)
            p.recip(den[:], den[:])
            p.tt(ob[:], num[:], den[:], ALU.mult)
            p.dma(yT[yrow:yrow + 128, t0:t0 + QB], ob[:])
        p.emit()


NF = 2 * L


def hyena_consts():
    c = {}
    f64 = np.float64
    n1 = np.arange(128)[:, None]; k1 = np.arange(128)[None, :]
    th = 2 * np.pi * n1 * k1 / 128
    c['C128f'] = np.cos(th).astype(NBF)
    c['S128fn'] = (-np.sin(th)).astype(NBF)
    k1c = np.arange(128)[:, None]; n1r = np.arange(64)[None, :]
    thi = 2 * np.pi * k1c * n1r / 128
    c['C128i'] = (np.cos(thi) / NF).astype(NBF)
    c['S128in'] = (-np.sin(thi) / NF).astype(NBF)
    n2 = np.arange(256)[:, None]; k2 = np.arange(256)[None, :]
    th2 = 2 * np.pi * n2 * k2 / 256
    def split(m):
        return np.ascontiguousarray(m.reshape(2, 128, 256).transpose(1, 0, 2))
    c['C256'] = split(np.cos(th2)).astype(NBF)
    c['S256'] = split(np.sin(th2)).astype(NBF)
    c['S256n'] = split(-np.sin(th2)).astype(NBF)
    k1c = np.arange(128, dtype=f64)[:, None]; n2r = np.arange(256, dtype=f64)[None, :]
    tw = 2 * np.pi * k1c * n2r / NF
    c['Twf_c'] = np.tile(np.cos(tw)[:, None, :], (1, 2, 1)).astype(np.float32)
    c['Twf_s'] = np.tile(np.sin(tw)[:, None, :], (1, 2, 1)).astype(np.float32)
    twi = tw.T
    tci = np.cos(twi).reshape(2, 128, 1, 128).transpose(1, 0, 2, 3)
    tsi = np.sin(twi).reshape(2, 128, 1, 128).transpose(1, 0, 2, 3)
    c['Twi_c'] = np.ascontiguousarray(np.tile(tci, (1, 1, 4, 1))).astype(np.float32)
    c['Twi_s'] = np.ascontiguousarray(np.tile(tsi, (1, 1, 4, 1))).astype(np.float32)
    idx = np.arange(NF)
    tau = np.where(idx < L, idx, NF - idx)
    pos = tau.astype(np.float32)
    t = (np.linspace(0.0, 1.0, L, dtype=np.float32))[np.minimum(tau, L - 1)]
    ang = (np.float32(2.0 * math.pi / L) * pos).astype(np.float32)
    freqs = np.linspace(1e-4, 15, 16, dtype=np.float32)
    z = np.concatenate([t[:, None], np.cos(ang[:, None] * freqs), -np.sin(ang[:, None] * freqs)], -1)
    c['zT'] = np.ascontiguousarray(z.T).astype(np.float32)
    tw_ = t.copy(); tw_[L] = 1e4
    c['trow'] = tw_.reshape(1, NF).astype(np.float32)
    max_decay = math.log(1e-2) / 0.3
    min_decay = math.log(1e-2) / 1.5
    deltas = np.abs(np.linspace(min_decay, max_decay, 1024, dtype=np.float32))
    c['ndelta'] = np.ascontiguousarray((-deltas).reshape(8, 128).T).astype(np.float32)
    return c

HY_SPECS = {'C128f': ([128, 128], BF16), 'S128fn': ([128, 128], BF16), 'C128i': ([128, 64], BF16),
            'S128in': ([128, 64], BF16), 'C256': ([128, 512], BF16), 'S256': ([128, 512], BF16),
            'S256n': ([128, 512], BF16), 'Twf_c': ([128, 512], F32), 'Twf_s': ([128, 512], F32),
            'Twi_c': ([128, 1024], F32), 'Twi_s': ([128, 1024], F32), 'zT': ([33, NF], F32),
            'trow': ([1, NF], F32), 'ndelta': ([128, 8], F32)}


def phase_hyconv3(g, projT, short_col, hyT, cts=None):
    nc = g.nc
    p = P(g)
    CW = 4096
    with contextlib.ExitStack() as st:
        sw = sb(st, nc, "c3_w", [128, 3, 24], F32)
        u = sb(st, nc, "c3_u", [128, CW + 2], F32)
        o = sb(st, nc, "c3_o", [128, CW], F32)
        p.dma(sw[:], short_col)
        for ct in (range(24) if cts is None else cts):
            for c0 in range(0, L, CW):
                lo = max(c0 - 1, 0); hi = min(c0 + CW + 1, L)
                if c0 == 0:
                    p.memset(u[:, 0:1], 0.0)
                if c0 + CW == L:
                    p.memset(u[:, CW + 1:CW + 2], 0.0)
                p.dma(u[:, lo - (c0 - 1):hi - (c0 - 1)], projT[ct * 128:(ct + 1) * 128, lo:hi])
                p.ts(o[:], u[:, 0:CW], sw[:, 0, ct:ct + 1], None, ALU.mult)
                p.stt(o[:], u[:, 1:CW + 1], sw[:, 1, ct:ct + 1], o[:], ALU.mult, ALU.add)
                p.stt(o[:], u[:, 2:CW + 2], sw[:, 2, ct:ct + 1], o[:], ALU.mult, ALU.add)
                p.dma(hyT[ct * 128:(ct + 1) * 128, c0:c0 + CW], o[:])
        p.emit()


def phase_hyfilter(g, hc, w1, b1col, w2, b2col, w3, filtb, cgs=None):
    nc = g.nc
    p = P(g)
    TW = 512
    NT = NF // TW
    PI = math.pi
    with contextlib.ExitStack() as st:
        w1s = sb(st, nc, "hf_w1", [33, 64], F32)
        w2s = sb(st, nc, "hf_w2", [64, 2, 64], F32)
        w3s = sb(st, nc, "hf_w3", [64, 4096], F32)
        bs = sb(st, nc, "hf_b", [64, 3], F32)
        zt = sb(st, nc, "hf_z", [33, TW], F32)
        ha = sb(st, nc, "hf_ha", [64, TW], F32)
        va = sb(st, nc, "hf_va", [64, TW], F32)
        ki = sb(st, nc, "hf_ki", [64, TW], mybir.dt.int32)
        kf = sb(st, nc, "hf_kf", [64, TW], F32)
        hid = sb(st, nc, "hf_hid", [64, NF], BF16)
        w3b = sb(st, nc, "hf_w3b", [64, 4096], BF16)
        trow = sb(st, nc, "hf_trow", [128, TW], F32)
        win = sb(st, nc, "hf_win", [128, TW], F32)
        nd = sb(st, nc, "hf_nd", [128, 8], F32)
        ff = sb(st, nc, "hf_ff", [128, TW], F32)
        part = sb(st, nc, "hf_part", [128, NT], F32)
        ssum = sb(st, nc, "hf_ssum", [128, 2], F32)
        fb = sb(st, nc, "hf_fb", [128, NF // 2], BF16)
        acc = ps(st, nc, "hf_ps", [128, TW], F32)
        p.dma(w1s[:], w1)
        p.dma(w2s[:], w2.rearrange("i a b -> a i b"))
        p.dma(w3s[:], w3)
        p.copy(w3b[:], w3s[:])
        p.dma(bs[:, 0:1], b1col)
        p.dma(bs[:, 1:3], b2col)
        p.dma(nd[:], hc['ndelta'][:, :])
        def sin_red(dst, src, bcol):
            p.ts(va[:], src, bcol, 1.0 / (2 * PI), ALU.add, ALU.mult)
            p.copy(ki[:], va[:])
            p.copy(kf[:], ki[:])
            p.tt(va[:], va[:], kf[:], ALU.subtract)
            p.act(dst, va[:], AF.Sin, scale=2 * PI)

        for ti in range(NT):
            cs = slice(ti * TW, (ti + 1) * TW)
            p.dma(zt[:], hc['zT'][:, cs])
            p.mm(acc[0:64, :], w1s[:], zt[:])
            sin_red(ha[:], acc[0:64, :], bs[:, 0:1])
            for i in range(2):
                p.mm(acc[0:64, :], w2s[:, i, :], ha[:])
                sin_red(hid[:, cs] if i == 1 else ha[:], acc[0:64, :], bs[:, 1 + i:2 + i])
        for o in range(2):
            for cg in (range(8) if cgs is None else cgs):
                for half in range(2):
                    for tj in range(NT // 2):
                        ti = half * (NT // 2) + tj
                        cs = slice(ti * TW, (ti + 1) * TW)
                        col = o * 2048 + half * 1024 + cg * 128
                        p.dma(trow[:], hc['trow'][:, cs].partition_broadcast(128))
                        p.mm(acc[:], w3b[:, col:col + 128], hid[:, cs])
                        p.act(win[:], trow[:], AF.Exp, scale=nd[:, cg:cg + 1])
                        p.tt(ff[:], acc[:], win[:], ALU.mult)
                        p.add('dve', lambda e, a=part[:, ti:ti + 1], b=ff[:]: e.tensor_reduce(
                            a, b, AX.X, ALU.add, apply_absolute_value=True))
                        p.copy(fb[:, tj * TW:(tj + 1) * TW], ff[:], eng='act')
                    p.dma(filtb[o, cg * 128:(cg + 1) * 128, half * (NF // 2):(half + 1) * (NF // 2)], fb[:])
                p.red(ssum[:, 0:1], part[:], ALU.add)
                p.recip(ssum[:, 1:2], ssum[:, 0:1])
                for half in range(2):
                    hs = slice(half * (NF // 2), (half + 1) * (NF // 2))
                    p.dma(fb[:], filtb[o, cg * 128:(cg + 1) * 128, hs])
                    p.ts(fb[:], fb[:], ssum[:, 1:2], None, ALU.mult)
                    p.dma(filtb[o, cg * 128:(cg + 1) * 128, hs], fb[:])
        p.emit()


def phase_hyena(g, cst, hc, hyT, filtb, bias_row, yT, nq=256):
    nc = g.nc
    p = P(g)
    with contextlib.ExitStack() as st:
        T = {}
        for nm in ('C128f', 'S128fn', 'C128i', 'S128in', 'C256', 'S256', 'S256n', 'Twf_c', 'Twf_s', 'Twi_c', 'Twi_s'):
            shp, dt = HY_SPECS[nm]
            T[nm] = sb(st, nc, "hy_" + nm, shp, dt)
            p.dma(T[nm][:], hc[nm][:, :])
        identb = sb(st, nc, "hy_idb", [128, 128], BF16)
        idf = sb(st, nc, "hy_idf", [128, 128], F32)
        biasb = sb(st, nc, "hy_bias", [64, 2048], F32)
        vin = sb(st, nc, "hy_v", [64, 1024], F32)
        x1 = sb(st, nc, "hy_x1", [64, 1024], F32)
        x2 = sb(st, nc, "hy_x2", [64, 1024], F32)
        zb = sb(st, nc, "hy_zb", [64, 1024], BF16)
        zc = sb(st, nc, "hy_zc", [64, 1024], F32)
        z2 = sb(st, nc, "hy_z2", [64, 1024], F32)
        ob = sb(st, nc, "hy_ob", [64, 1024], BF16)
        fx = sb(st, nc, "hy_fx", [128, 1024], BF16)
        Br = sb(st, nc, "hy_Br", [128, 1024], BF16)
        Bi = sb(st, nc, "hy_Bi", [128, 1024], BF16)
        BTr = sb(st, nc, "hy_BTr", [128, 1024], BF16)
        BTi = sb(st, nc, "hy_BTi", [128, 1024], BF16)
        Fr = sb(st, nc, "hy_Fr", [128, 1024], F32)
        Fi = sb(st, nc, "hy_Fi", [128, 1024], F32)
        Yr = sb(st, nc, "hy_Yr", [128, 1024], BF16)
        Yi = sb(st, nc, "hy_Yi", [128, 1024], BF16)
        Dr = sb(st, nc, "hy_Dr", [128, 1024], BF16)
        Di = sb(st, nc, "hy_Di", [128, 1024], BF16)
        DTr = sb(st, nc, "hy_DTr", [128, 1024], BF16)
        DTi = sb(st, nc, "hy_DTi", [128, 1024], BF16)
        t1 = sb(st, nc, "hy_t1", [128, 512], F32)
        t2 = sb(st, nc, "hy_t2", [128, 512], F32)
        pa = ps(st, nc, "hy_pa", [128, 512], F32)
        pb = ps(st, nc, "hy_pb", [128, 512], F32)
        tp1 = ps(st, nc, "hy_tp1", [128, 512], BF16)
        tp2 = ps(st, nc, "hy_tp2", [128, 512], BF16)
        p.dma(idf[:], cst['ident'][:, :])
        p.copy(identb[:], idf[:])
        p.dma(biasb[:], bias_row.partition_broadcast(64))

        def cmul(outr, outi, ar, ai, br, bi, conj_b=False):
            p.tt(t1[:], ar, br, ALU.mult)
            p.tt(t2[:], ai, bi, ALU.mult)
            p.tt(outr, t1[:], t2[:], ALU.add if conj_b else ALU.subtract)
            p.tt(t1[:], ai, br, ALU.mult)
            p.tt(t2[:], ar, bi, ALU.mult)
            p.tt(outi, t1[:], t2[:], ALU.subtract if conj_b else ALU.add)

        def fwd_fft(X, K1, consume):
            for pr in range(2):
                cs = slice(pr * 512, (pr + 1) * 512)
                p.mm(pa[:], T['C128f'][0:K1, :], X[0:K1, cs])
                p.mm(pb[:], T['S128fn'][0:K1, :], X[0:K1, cs], nowait=True)
                cmul(Br[:, cs], Bi[:, cs], pa[:], pb[:], T['Twf_c'][:], T['Twf_s'][:], conj_b=True)
            for (src, dst, tp) in ((Br, BTr, tp1), (Bi, BTi, tp2)):
                for half in range(2):
                    for c in range(4):
                        o0 = c * 256 + half * 128
                        p.tr(tp[:, c * 128:(c + 1) * 128], src[:, o0:o0 + 128], identb[:], nowait=(c > 0))
                    p.copy(dst[:, half * 512:(half + 1) * 512], tp[:], eng='act')
            for kt in range(2):
                seq = [(pa, 'C256', BTr), (pa, 'S256', BTi), (pb, 'C256', BTi), (pb, 'S256n', BTr)]
                for si, (dst, tn, src) in enumerate(seq):
                    for half in range(2):
                        w0 = half * 256 + kt * 128
                        p.mm(dst[:], T[tn][:, w0:w0 + 128], src[:, half * 512:(half + 1) * 512],
                             start=(si % 2 == 0 and half == 0), stop=(si % 2 == 1 and half == 1),
                             nowait=(si + half > 0))
                consume(kt, pa, pb)

        def conv(zsrc_b, o_, ch0):
            p.dma(fx[:].rearrange("p (c n) -> p c n", n=256),
                  filtb[o_, ch0:ch0 + 4, :].rearrange("c (n1 n2) -> n1 c n2", n2=256))

            def keep(kt, xr, xi):
                p.copy(Fr[:, kt * 512:(kt + 1) * 512], xr[:], eng='act')
                p.copy(Fi[:, kt * 512:(kt + 1) * 512], xi[:])
            fwd_fft(fx, 128, keep)

            def prod(kt, xr, xi):
                ks = slice(kt * 512, (kt + 1) * 512)
                cmul(Yr[:, ks], Yi[:, ks], xr[:], xi[:], Fr[:, ks], Fi[:, ks])
            fwd_fft(zsrc_b, 64, prod)
            for nt in range(2):
                seq = [(pa, 'C256', Yr), (pa, 'S256n', Yi), (pb, 'C256', Yi), (pb, 'S256', Yr)]
                for si, (dst, tn, src) in enumerate(seq):
                    for kt in range(2):
                        w0 = kt * 256 + nt * 128
                        p.mm(dst[:], T[tn][:, w0:w0 + 128], src[:, kt * 512:(kt + 1) * 512],
                             start=(si % 2 == 0 and kt == 0), stop=(si % 2 == 1 and kt == 1),
                             nowait=(si + kt > 0))
                ns = slice(nt * 512, (nt + 1) * 512)
                cmul(Dr[:, ns], Di[:, ns], pa[:], pb[:], T['Twi_c'][:, ns], T['Twi_s'][:, ns])
            for (src, dst, tp) in ((Dr, DTr, tp1), (Di, DTi, tp2)):
                for c in range(4):
                    for nt in range(2):
                        o0 = nt * 512 + c * 128
                        p.tr(tp[:, nt * 128:(nt + 1) * 128], src[:, o0:o0 + 128], identb[:], nowait=(nt > 0))
                    p.copy(dst[:, c * 256:(c + 1) * 256], tp[:, 0:256], eng='act')
            for pr in range(2):
                cs = slice(pr * 512, (pr + 1) * 512)
                p.mm(pa[0:64, :], T['C128i'][:], DTr[:, cs], start=True, stop=False)
                p.mm(pa[0:64, :], T['S128in'][:], DTi[:, cs], start=False, stop=True, nowait=True)
                p.copy(zc[:, cs], pa[0:64, :])

        lay = "c (n1 n2) -> n1 c n2"
        v3 = lambda t: t[:].rearrange("p (c n) -> p c n", n=256)
        for q in range(nq):
            ch0 = q * 4
            p.dma(v3(vin), hyT[ch0:ch0 + 4, :].rearrange(lay, n2=256))
            p.dma(v3(x1), hyT[1024 + ch0:1024 + ch0 + 4, :].rearrange(lay, n2=256), nowait=True)
            p.dma(v3(x2), hyT[2048 + ch0:2048 + ch0 + 4, :].rearrange(lay, n2=256), nowait=True)
            p.copy(zb[:], vin[:])
            conv(zb, 0, ch0)
            for c in range(4):
                cs = slice(c * 256, (c + 1) * 256)
                p.stt(z2[:, cs], vin[:, cs], biasb[:, ch0 + c:ch0 + c + 1], zc[:, cs], ALU.mult, ALU.add)
            p.tt(z2[:], z2[:], x1[:], ALU.mult)
            p.copy(zb[:], z2[:])
            conv(zb, 1, ch0)
            for c in range(4):
                cs = slice(c * 256, (c + 1) * 256)
                p.stt(vin[:, cs], z2[:, cs], biasb[:, 1024 + ch0 + c:1024 + ch0 + c + 1], zc[:, cs],
                      ALU.mult, ALU.add)
            p.tt(ob[:], vin[:], x2[:], ALU.mult)
            p.dma(yT[ch0:ch0 + 4, :].rearrange(lay, n2=256), v3(ob))
        p.emit()


def prep_hy_consts():
    hc = hyena_consts()
    out = {}
    for k, (shp, dt) in HY_SPECS.items():
        out[k] = np.ascontiguousarray(hc[k]).reshape(shp)
    return out


def declare_consts(nc):
    cst = {nm: nc.dram_tensor("c_" + nm, [128, 128], F32, kind="ExternalInput").ap() for nm in CONST_NAMES}
    hc = {nm: nc.dram_tensor("h_" + nm, shp, dt, kind="ExternalInput").ap() for nm, (shp, dt) in HY_SPECS.items()}
    btab = nc.dram_tensor("c_btab", [8, 3, 128, 256], F32, kind="ExternalInput").ap()
    return cst, hc, btab


def const_inputs():
    d = {"c_" + k: v for k, v in make_consts().items()}
    d.update({"h_" + k: v for k, v in prep_hy_consts().items()})
    d["c_btab"] = attn_bias_tables()
    return d


def mixers(g, cst, hc, btab, projT, prm, scr, yT, sel=None):
    sel = sel or {}
    if sel.get('hy', True):
        nq = sel.get('hy_nq', 256)
        cts = None if nq == 256 else [0, 8, 16]
        cgs = None if nq == 256 else [0]
        phase_hyconv3(g, projT, prm['short_col'], scr['hyT'], cts=cts)
        phase_hyfilter(g, hc, prm['hy_w1'], prm['hy_b1'], prm['hy_w2'], prm['hy_b2'], prm['hy_w3'], scr['filtb'], cgs=cgs)
        phase_hyena(g, cst, hc, scr['hyT'], scr['filtb'], prm['hy_bias'], yT, nq=nq)
    for h in sel.get('ret', range(4)):
        srcs = [(None, None, None, prm['ret_decay'][0:1, d * 4 + h:d * 4 + h + 1]) for d in range(2)]
        phase_linattn(g, cst, projT, 3072 + 128 * h, 3584 + 128 * h, 4096 + 256 * h, 128.0 ** -0.5, srcs,
                      scr['ydir'], False)
        phase_combine(g, cst, projT, scr['ydir'], 5120 + 256 * h, yT, 1024 + 256 * h, False, None)
    for h in sel.get('att', range(8)):
        phase_attn(g, cst, projT, 6144 + 128 * h, 7168 + 128 * h, 8192 + 128 * h, prm['att_gain'], btab[h], yT,
                   2048 + 128 * h)
    for h in sel.get('ml', range(4)):
        srcs = []
        for d in range(2):
            ri = 12288 + d * 8 + h
            rf = 12288 + d * 8 + 4 + h
            srcs.append((projT[ri, :], projT[rf, :], prm['ml_gate_bias'][0:1, d * 8 + h:d * 8 + h + 1],
                         prm['ml_gate_bias'][0:1, d * 8 + 4 + h:d * 8 + 4 + h + 1]))
        phase_linattn(g, cst, projT, 9216 + 128 * h, 9728 + 128 * h, 10240 + 256 * h, 128.0 ** -0.5, srcs,
                      scr['ydir'], True)
        phase_combine(g, cst, projT, scr['ydir'], 11264 + 256 * h, yT, 3072 + 256 * h, True,
                      prm['ml_gain'][:, 2 * h:2 * h + 2])


PRM_SPECS = {'table_col': [128, 192], 'short_col': [128, 3, 24], 'hy_w1': [33, 64], 'hy_b1': [64, 1],
             'hy_w2': [2, 64, 64], 'hy_b2': [64, 2], 'hy_w3': [64, 4096], 'hy_bias': [1, 2048],
             'ret_decay': [1, 8], 'att_gain': [128, 2], 'ml_gate_bias': [1, 16], 'ml_gain': [128, 8]}


def prm_inputs(inp, l):
    f = lambda a: np.ascontiguousarray(np.asarray(a, dtype=np.float32))
    return {
        'table_col': f(inp['ada_table'][l].reshape(192, 128).T),
        'short_col': f(inp['hy_short'][l].reshape(3, 24, 128).transpose(2, 0, 1)),
        'hy_w1': f(inp['hy_w1'][l]), 'hy_b1': f(inp['hy_b1'][l].reshape(64, 1)),
        'hy_w2': f(inp['hy_w2'][l]), 'hy_b2': f(inp['hy_b2'][l].T),
        'hy_w3': f(inp['hy_w3'][l]), 'hy_bias': f(inp['hy_bias'][l].reshape(1, 2048)),
        'ret_decay': f(inp['ret_decay'][l].reshape(1, 8)),
        'att_gain': f(inp['att_qk_gain'][l].T),
        'ml_gate_bias': f(inp['ml_gate_bias'][l].reshape(1, 16)),
        'ml_gain': f(inp['ml_norm_gain'][l].reshape(4, 2, 128).transpose(2, 0, 1).reshape(128, 8)),
    }


def phase_modl(g, modsh, table_col, modl):
    nc = g.nc
    p = P(g)
    with contextlib.ExitStack() as st:
        a = sb(st, nc, "ml_a", [128, 192], F32)
        b = sb(st, nc, "ml_b", [128, 192], F32)
        p.dma(a[:], modsh[:, :])
        p.dma(b[:], table_col[:, :])
        p.tt(a[:], a[:], b[:], ALU.add)
        p.dma(modl[:, :], a[:])
        p.emit()


def build_layer():
    nc = bass.Bass("TRN2", target_bir_lowering=False)
    ext = lambda n, s, d=F32: nc.dram_tensor(n, list(s), d, kind="ExternalInput").ap()
    xT = ext("xT", [D, L])
    modsh = ext("modsh", [128, 192])
    w_in = ext("w_in", [D, D_IN]); w_out = ext("w_out", [D, D])
    w1 = ext("ffn_w1", [D, FH]); w3 = ext("ffn_w3", [D, FH]); w2 = ext("ffn_w2", [FH, D])
    prm = {k: ext("p_" + k, s) for k, s in PRM_SPECS.items()}
    cst, hc, btab = declare_consts(nc)
    xo = nc.dram_tensor("xo", [D, L], F32, kind="ExternalOutput").ap()
    it = lambda n, s, d: nc.dram_tensor(n, list(s), d).ap()
    wb_in = it("wb_in", [D, D_IN], BF16); wb_out = it("wb_out", [D, D], BF16)
    wb1 = it("wb1", [D, FH], BF16); wb3 = it("wb3", [D, FH], BF16); wb2 = it("wb2", [FH, D], BF16)
    hT = it("hT", [D, L], BF16)
    projT = RowSplit([it("projA", [3072, L], F32), it("projB", [3072, L], F32), it("projC", [3072, L], F32),
                      it("projD", [3088, L], F32)], [0, 3072, 6144, 9216, 12304])
    yT = it("yT", [D, L], BF16)
    uT = RowSplit([it("uTa", [5504, L], BF16), it("uTb", [5504, L], BF16)], [0, 5504, 11008])
    x1T = RowSplit([it("x1Ta", [2048, L], F32), it("x1Tb", [2048, L], F32)], [0, 2048, 4096])
    modl = it("modl", [128, 192], F32)
    scr = {'hyT': it("hyT", [3072, L], F32), 'filtb': it("filtb", [2, 1024, NF], BF16),
           'ydir': [it("ydir0", [L, 256], F32), it("ydir1", [L, 256], F32)]}
    with nc.semaphore("G") as sem:
        g = G(nc, sem)
        phase_modl(g, modsh, prm['table_col'], modl)
        phase_cast(g, w_in, wb_in, D, D_IN)
        phase_cast(g, w_out, wb_out, D, D)
        phase_cast(g, w1, wb1, D, FH)
        phase_cast(g, w3, wb3, D, FH)
        phase_cast(g, w2, wb2, FH, D)
        phase_norm(g, xT, hT, modl, 0, 32, L)
        phase_dense(g, hT, wb_in, projT, D, D_IN, L, 'copy')
        mixers(g, cst, hc, btab, projT, prm, scr, yT)
        phase_dense(g, yT, wb_out, x1T, D, D, L, 'resid', resT=xT, modl=modl, gate_j=64)
        phase_norm(g, x1T, hT, modl, 96, 128, L)
        phase_dense(g, hT, wb1, uT, D, FH, L, 'swiglu', Wb2=wb3, out_dt=BF16)
        phase_dense(g, uT, wb2, xo, FH, D, L, 'resid', resT=x1T, modl=modl, gate_j=160)
    return nc


def build_mod():
    nc = bass.Bass("TRN2", target_bir_lowering=False)
    c_col = nc.dram_tensor("c_col", [128, 32], F32, kind="ExternalInput").ap()
    ada_w = nc.dram_tensor("ada_w", [D, 6 * D], F32, kind="ExternalInput").ap()
    ada_b = nc.dram_tensor("ada_b", [128, 192], F32, kind="ExternalInput").ap()
    modo = nc.dram_tensor("modo", [128, 192], F32, kind="ExternalOutput").ap()
    with nc.semaphore("G") as sem:
        g = G(nc, sem)
        phase_mod(g, c_col, ada_w, ada_b, modo)
    return nc


def kernel(**inp):
    f = lambda a: np.ascontiguousarray(np.asarray(a, dtype=np.float32))
    ncm = build_mod()
    r = run_bass_kernel_spmd(ncm, [{"c_col": f(np.asarray(inp['c'])[0].reshape(32, 128).T), "ada_w": f(inp['ada_w']),
                                   "ada_b": f(np.asarray(inp['ada_b']).reshape(192, 128).T)}], core_ids=[0])
    modsh = r.results[0]["modo"]
    nc = build_layer()
    consts = const_inputs()
    xT = f(np.asarray(inp['x'])[0].T)
    for l in range(DEPTH):
        m = {"xT": xT, "modsh": modsh, "w_in": f(inp['w_in'][l]), "w_out": f(inp['w_out'][l]),
             "ffn_w1": f(inp['ffn_w1'][l]), "ffn_w3": f(inp['ffn_w3'][l]), "ffn_w2": f(inp['ffn_w2'][l])}
        m.update({"p_" + k: v for k, v in prm_inputs(inp, l).items()})
        m.update(consts)
        r = run_bass_kernel_spmd(nc, [m], core_ids=[0])
        xT = r.results[0]["xo"]
    return np.ascontiguousarray(xT.T)[None].astype(np.float32)
```
